# Optimizing a Trainium2 kernel written in Bass

```python
import jax, jax.numpy as jnp
from jax import lax
import numpy as np

D_MODEL = 1024
BATCH = 4
SEQ = 4096
DEPTH = 4
DEC_BATCH = 128
DEC_SEQ = 4
PAST_LEN = 8192
PAGE_SIZE = 128

N_MIXERS = 3
N_CONV_LAYERS = (DEPTH + 2) // 3
N_MLA_LAYERS = (DEPTH + 1) // 3
N_RET_LAYERS = DEPTH // 3

CONV_WIDTH = 31
MLA_HEADS = 8
MLA_Q_LORA = 384
MLA_KV_LORA = 256
MLA_NOPE = 128
MLA_ROPE = 64
MLA_V = 128
MLA_Q_BLOCK = 128
MLA_SCALE = (MLA_NOPE + MLA_ROPE) ** -0.5
ROPE_BASE = 10000.0
RET_HEADS = D_MODEL // 256
RET_DK = D_MODEL // RET_HEADS
RET_DV = 2 * RET_DK
RET_CHUNK = 128
D_FF = -(-8 * D_MODEL // (3 * 256)) * 256
EPS = 1e-6
NEG_INF = -1e30

kernel_name = "hybrid_conv_mla_retention_decode_step"


def _rmsnorm(x, g):
    xf = x.astype(jnp.float32)
    xf = xf * lax.rsqrt(jnp.mean(xf * xf, axis=-1, keepdims=True) + EPS)
    return (xf * g.astype(jnp.float32)).astype(x.dtype)


def _layernorm(x, g, b):
    xf = x.astype(jnp.float32)
    mu = jnp.mean(xf, axis=-1, keepdims=True)
    xc = xf - mu
    xf = xc * lax.rsqrt(jnp.mean(xc * xc, axis=-1, keepdims=True) + EPS)
    return (xf * g.astype(jnp.float32) + b.astype(jnp.float32)).astype(x.dtype)


def _rope(x, pos, inv_freq):
    half = x.shape[-1] // 2
    ang = pos.astype(jnp.float32)[:, None] * inv_freq[None, :]
    cos = jnp.cos(ang)[None, :, None, :].astype(x.dtype)
    sin = jnp.sin(ang)[None, :, None, :].astype(x.dtype)
    x1, x2 = x[..., :half], x[..., half:]
    return jnp.concatenate([x1 * cos - x2 * sin, x2 * cos + x1 * sin], axis=-1)


def _mla_inv_freq():
    return ROPE_BASE ** (-jnp.arange(0, MLA_ROPE, 2, dtype=jnp.float32) / MLA_ROPE)


def _ret_inv_freq():
    return 1.0 / (ROPE_BASE ** jnp.linspace(0.0, 1.0, RET_DK // 2, dtype=jnp.float32))


def _swiglu(h, w1, w3, w2):
    return (jax.nn.silu(h @ w1) * (h @ w3)) @ w2


def _conv_module(h, buf_prev, w_pw1, b_pw1, w_dw, b_dw, ln_g, ln_b, w_pw2, b_pw2):
    a = h @ w_pw1 + b_pw1
    u = a[..., :D_MODEL] * jax.nn.sigmoid(a[..., D_MODEL:])
    buf = jnp.concatenate([buf_prev.astype(u.dtype), u], axis=1)
    c = lax.conv_general_dilated(buf, w_dw[:, None, :].astype(buf.dtype), (1,), 'VALID',
                                 dimension_numbers=('NWC', 'WIO', 'NWC'),
                                 feature_group_count=D_MODEL) + b_dw
    c = jax.nn.silu(_layernorm(c, ln_g, ln_b))
    return c @ w_pw2 + b_pw2, buf[:, -(CONV_WIDTH - 1):]


def _mla_project(h, pos, w_dq, g_q, w_uq, w_dkv, g_kv, w_uk):
    B, T, _ = h.shape
    inv = _mla_inv_freq()
    cq = _rmsnorm(h @ w_dq, g_q)
    q = (cq @ w_uq).reshape(B, T, MLA_HEADS, MLA_NOPE + MLA_ROPE)
    q_rope = _rope(q[..., MLA_NOPE:], pos, inv)
    q_lat = jnp.einsum('bthn,rhn->bthr', q[..., :MLA_NOPE], w_uk)
    a = h @ w_dkv
    c_kv = _rmsnorm(a[..., :MLA_KV_LORA], g_kv)
    k_rope = _rope(a[..., MLA_KV_LORA:][:, :, None, :], pos, inv)[:, :, 0, :]
    return q_lat, q_rope, c_kv, k_rope


def _mla_scores(q_lat, q_rope, c_kv, k_rope):
    s = (jnp.einsum('bthr,bsr->bhts', q_lat, c_kv).astype(jnp.float32)
         + jnp.einsum('bthe,bse->bhts', q_rope, k_rope).astype(jnp.float32))
    return s * MLA_SCALE


def _mla_attend_prompt(q_lat, q_rope, c_kv, k_rope):
    B, S = q_lat.shape[:2]
    nb = S // MLA_Q_BLOCK
    qb = q_lat.reshape(B, nb, MLA_Q_BLOCK, MLA_HEADS, MLA_KV_LORA).swapaxes(0, 1)
    rb = q_rope.reshape(B, nb, MLA_Q_BLOCK, MLA_HEADS, MLA_ROPE).swapaxes(0, 1)
    k_pos = jnp.arange(S)

    def block(args):
        q_l, q_r, i = args
        q_pos = i * MLA_Q_BLOCK + jnp.arange(MLA_Q_BLOCK)
        s = _mla_scores(q_l, q_r, c_kv, k_rope)
        s = jnp.where((k_pos[None, :] <= q_pos[:, None])[None, None], s, NEG_INF)
        p = jax.nn.softmax(s, axis=-1).astype(c_kv.dtype)
        return jnp.einsum('bhts,bsr->bthr', p, c_kv)

    out = lax.map(block, (qb, rb, jnp.arange(nb)))
    return out.swapaxes(0, 1).reshape(B, S, MLA_HEADS, MLA_KV_LORA)


def _mla_attend_sample(q_lat, q_rope, c_new, kr_new, c_past, kr_past):
    T = q_lat.shape[1]
    P = c_past.shape[1]
    s_past = _mla_scores(q_lat, q_rope, c_past, kr_past)
    s_new = _mla_scores(q_lat, q_rope, c_new, kr_new)
    causal = jnp.tril(jnp.ones((T, T), dtype=bool))
    s_new = jnp.where(causal[None, None], s_new, NEG_INF)
    p = jax.nn.softmax(jnp.concatenate([s_past, s_new], axis=-1), axis=-1).astype(c_new.dtype)
    return (jnp.einsum('bhts,bsr->bthr', p[..., :P], c_past)
            + jnp.einsum('bhts,bsr->bthr', p[..., P:], c_new))


def _mla_out(out_lat, w_uv, w_o):
    B, T = out_lat.shape[:2]
    o = jnp.einsum('bthr,rhv->bthv', out_lat, w_uv).reshape(B, T, MLA_HEADS * MLA_V)
    return o @ w_o


def _ret_log_gamma():
    return jnp.log(1.0 - 2.0 ** (-5.0 - jnp.arange(RET_HEADS, dtype=jnp.float32)))


def _ret_qkvg(h, pos, w_q, w_k, w_v, w_g):
    B, T, _ = h.shape
    inv = _ret_inv_freq()
    q = _rope((h @ w_q).reshape(B, T, RET_HEADS, RET_DK), pos, inv)
    k = _rope((h @ w_k).reshape(B, T, RET_HEADS, RET_DK), pos, inv) * (RET_DK ** -0.5)
    v = (h @ w_v).reshape(B, T, RET_HEADS, RET_DV)
    g = h @ w_g
    return q.astype(jnp.float32), k.astype(jnp.float32), v.astype(jnp.float32), g


def _ret_chunk(q, k, v, state, log_gamma):
    L = q.shape[1]
    n = jnp.arange(L, dtype=jnp.float32)
    diff = n[:, None] - n[None, :]
    decay = jnp.exp(jnp.where(diff[None] >= 0, diff[None] * log_gamma[:, None, None], -jnp.inf))
    inner = jnp.einsum('bnhd,bmhd->bhnm', q, k) * decay[None]
    o = jnp.einsum('bhnm,bmhe->bnhe', inner, v)
    cross = jnp.exp((n[:, None] + 1.0) * log_gamma[None, :])
    o = o + jnp.einsum('bnhd,bhde->bnhe', q, state) * cross[None, :, :, None]
    k_dec = k * jnp.exp((L - 1.0 - n)[:, None] * log_gamma[None, :])[None, :, :, None]
    new_state = (state * jnp.exp(L * log_gamma)[None, :, None, None]
                 + jnp.einsum('bmhd,bmhe->bhde', k_dec, v))
    return o, new_state


def _ret_out(o, g, gn_g, gn_b, w_o):
    B, T = o.shape[:2]
    mu = jnp.mean(o, axis=-1, keepdims=True)
    oc = o - mu
    on = oc * lax.rsqrt(jnp.mean(oc * oc, axis=-1, keepdims=True) + EPS)
    on = on.reshape(B, T, RET_HEADS * RET_DV) * gn_g.astype(jnp.float32) + gn_b.astype(jnp.float32)
    return (jax.nn.silu(g) * on.astype(g.dtype)) @ w_o


def setup_inputs(seed: int = 0) -> dict:
    key = jax.random.key(seed)
    ks = iter(jax.random.split(key, 64))
    f32 = jnp.float32

    def nrm(shape, fan_in):
        return jax.random.normal(next(ks), shape, f32) * (fan_in ** -0.5)

    def gain(shape):
        return 1.0 + 0.05 * jax.random.normal(next(ks), shape, f32)

    def bias(shape):
        return 0.02 * jax.random.normal(next(ks), shape, f32)

    n_pages = PAST_LEN // PAGE_SIZE
    n_used = DEC_BATCH * n_pages
    n_phys = n_used + (n_used + 3) // 4
    perm = jax.random.permutation(next(ks), n_phys)
    page_table = perm[:n_used].reshape(DEC_BATCH, n_pages).astype(jnp.int32)

    return {
        'x_prompt': jax.random.normal(next(ks), (BATCH, SEQ, D_MODEL), f32),
        'x_sample': jax.random.normal(next(ks), (DEC_BATCH, DEC_SEQ, D_MODEL), f32),
        'state_conv': jax.random.normal(next(ks), (N_CONV_LAYERS, DEC_BATCH, CONV_WIDTH - 1, D_MODEL), f32),
        'cache_mla_latent': jax.random.normal(next(ks), (N_MLA_LAYERS, n_phys, PAGE_SIZE, MLA_KV_LORA), f32),
        'cache_mla_krope': jax.random.normal(next(ks), (N_MLA_LAYERS, n_phys, PAGE_SIZE, MLA_ROPE), f32),
        'state_ret': jax.random.normal(next(ks), (N_RET_LAYERS, DEC_BATCH, RET_HEADS, RET_DK, RET_DV), f32),
        'page_table': page_table,
        'norm_mix': gain((DEPTH, D_MODEL)),
        'norm_ffn': gain((DEPTH, D_MODEL)),
        'norm_final': gain((D_MODEL,)),
        'conv_w_pw1': nrm((N_CONV_LAYERS, D_MODEL, 2 * D_MODEL), D_MODEL),
        'conv_b_pw1': bias((N_CONV_LAYERS, 2 * D_MODEL)),
        'conv_w_dw': nrm((N_CONV_LAYERS, CONV_WIDTH, D_MODEL), CONV_WIDTH),
        'conv_b_dw': bias((N_CONV_LAYERS, D_MODEL)),
        'conv_ln_g': gain((N_CONV_LAYERS, D_MODEL)),
        'conv_ln_b': bias((N_CONV_LAYERS, D_MODEL)),
        'conv_w_pw2': nrm((N_CONV_LAYERS, D_MODEL, D_MODEL), D_MODEL),
        'conv_b_pw2': bias((N_CONV_LAYERS, D_MODEL)),
        'mla_w_dq': nrm((N_MLA_LAYERS, D_MODEL, MLA_Q_LORA), D_MODEL),
        'mla_g_q': gain((N_MLA_LAYERS, MLA_Q_LORA)),
        'mla_w_uq': nrm((N_MLA_LAYERS, MLA_Q_LORA, MLA_HEADS * (MLA_NOPE + MLA_ROPE)), MLA_Q_LORA),
        'mla_w_dkv': nrm((N_MLA_LAYERS, D_MODEL, MLA_KV_LORA + MLA_ROPE), D_MODEL),
        'mla_g_kv': gain((N_MLA_LAYERS, MLA_KV_LORA)),
        'mla_w_uk': nrm((N_MLA_LAYERS, MLA_KV_LORA, MLA_HEADS, MLA_NOPE), MLA_KV_LORA),
        'mla_w_uv': nrm((N_MLA_LAYERS, MLA_KV_LORA, MLA_HEADS, MLA_V), MLA_KV_LORA),
        'mla_w_o': nrm((N_MLA_LAYERS, MLA_HEADS * MLA_V, D_MODEL), MLA_HEADS * MLA_V),
        'ret_w_q': nrm((N_RET_LAYERS, D_MODEL, RET_HEADS * RET_DK), D_MODEL),
        'ret_w_k': nrm((N_RET_LAYERS, D_MODEL, RET_HEADS * RET_DK), D_MODEL),
        'ret_w_v': nrm((N_RET_LAYERS, D_MODEL, RET_HEADS * RET_DV), D_MODEL),
        'ret_w_g': nrm((N_RET_LAYERS, D_MODEL, RET_HEADS * RET_DV), D_MODEL),
        'ret_gn_g': gain((N_RET_LAYERS, RET_HEADS * RET_DV)),
        'ret_gn_b': bias((N_RET_LAYERS, RET_HEADS * RET_DV)),
        'ret_w_o': nrm((N_RET_LAYERS, RET_HEADS * RET_DV, D_MODEL), RET_HEADS * RET_DV),
        'ffn_w1': nrm((DEPTH, D_MODEL, D_FF), D_MODEL),
        'ffn_w3': nrm((DEPTH, D_MODEL, D_FF), D_MODEL),
        'ffn_w2': nrm((DEPTH, D_FF, D_MODEL), D_FF),
    }


def reference(x_prompt, x_sample, state_conv, cache_mla_latent, cache_mla_krope, state_ret, page_table,
              norm_mix, norm_ffn, norm_final,
              conv_w_pw1, conv_b_pw1, conv_w_dw, conv_b_dw, conv_ln_g, conv_ln_b, conv_w_pw2, conv_b_pw2,
              mla_w_dq, mla_g_q, mla_w_uq, mla_w_dkv, mla_g_kv, mla_w_uk, mla_w_uv, mla_w_o,
              ret_w_q, ret_w_k, ret_w_v, ret_w_g, ret_gn_g, ret_gn_b, ret_w_o,
              ffn_w1, ffn_w3, ffn_w2):
    B, S, _ = x_prompt.shape
    DB, T, _ = x_sample.shape
    past_len = page_table.shape[1] * PAGE_SIZE
    pos_p = jnp.arange(S)
    pos_s = past_len + jnp.arange(T)
    log_gamma = _ret_log_gamma()

    xp, xs = x_prompt, x_sample
    conv_p, conv_s = [], []
    lat_p, kr_p, lat_s, kr_s = [], [], [], []
    ret_p, ret_s = [], []

    for i in range(DEPTH):
        j = i // N_MIXERS
        kind = i % N_MIXERS
        hp = _rmsnorm(xp, norm_mix[i])
        hs = _rmsnorm(xs, norm_mix[i])
        if kind == 0:
            cw = (conv_w_pw1[j], conv_b_pw1[j], conv_w_dw[j], conv_b_dw[j],
                  conv_ln_g[j], conv_ln_b[j], conv_w_pw2[j], conv_b_pw2[j])
            mp, bp = _conv_module(hp, jnp.zeros((B, CONV_WIDTH - 1, D_MODEL), hp.dtype), *cw)
            ms, bs = _conv_module(hs, state_conv[j], *cw)
            conv_p.append(bp)
            conv_s.append(bs.astype(state_conv.dtype))
        elif kind == 1:
            pw = (mla_w_dq[j], mla_g_q[j], mla_w_uq[j], mla_w_dkv[j], mla_g_kv[j], mla_w_uk[j])
            ql, qr, ck, kr = _mla_project(hp, pos_p, *pw)
            mp = _mla_out(_mla_attend_prompt(ql, qr, ck, kr), mla_w_uv[j], mla_w_o[j])
            lat_p.append(ck)
            kr_p.append(kr)
            ql, qr, ck, kr = _mla_project(hs, pos_s, *pw)
            c_past = cache_mla_latent[j][page_table].reshape(DB, past_len, MLA_KV_LORA)
            r_past = cache_mla_krope[j][page_table].reshape(DB, past_len, MLA_ROPE)
            ms = _mla_out(_mla_attend_sample(ql, qr, ck, kr, c_past.astype(ck.dtype), r_past.astype(kr.dtype)),
                          mla_w_uv[j], mla_w_o[j])
            lat_s.append(ck)
            kr_s.append(kr)
        else:
            rw = (ret_w_q[j], ret_w_k[j], ret_w_v[j], ret_w_g[j])
            q, k, v, g = _ret_qkvg(hp, pos_p, *rw)
            nc = S // RET_CHUNK

            def chunks(a):
                return a.reshape(B, nc, RET_CHUNK, *a.shape[2:]).swapaxes(0, 1)

            def step(st, qkv):
                o_c, st = _ret_chunk(qkv[0], qkv[1], qkv[2], st, log_gamma)
                return st, o_c

            st0 = jnp.zeros((B, RET_HEADS, RET_DK, RET_DV), jnp.float32)
            st_p, o = lax.scan(step, st0, (chunks(q), chunks(k), chunks(v)))
            o = o.swapaxes(0, 1).reshape(B, S, RET_HEADS, RET_DV)
            mp = _ret_out(o, g, ret_gn_g[j], ret_gn_b[j], ret_w_o[j])
            ret_p.append(st_p.astype(xp.dtype))
            q, k, v, g = _ret_qkvg(hs, pos_s, *rw)
            o, st_s = _ret_chunk(q, k, v, state_ret[j].astype(jnp.float32), log_gamma)
            ms = _ret_out(o, g, ret_gn_g[j], ret_gn_b[j], ret_w_o[j])
            ret_s.append(st_s.astype(state_ret.dtype))
        xp = xp + mp.astype(xp.dtype)
        xs = xs + ms.astype(xs.dtype)
        xp = xp + _swiglu(_rmsnorm(xp, norm_ffn[i]), ffn_w1[i], ffn_w3[i], ffn_w2[i])
        xs = xs + _swiglu(_rmsnorm(xs, norm_ffn[i]), ffn_w1[i], ffn_w3[i], ffn_w2[i])

    y_prompt = _rmsnorm(xp, norm_final)
    y_sample = _rmsnorm(xs, norm_final)
    return (y_prompt, y_sample,
            jnp.stack(conv_p), jnp.stack(conv_s),
            jnp.stack(lat_p), jnp.stack(kr_p), jnp.stack(lat_s), jnp.stack(kr_s),
            jnp.stack(ret_p), jnp.stack(ret_s))
```

```python
import numpy as np
import concourse.bass as bass
import concourse.mybir as mybir
from concourse.bass_utils import run_bass_kernel_spmd

F32 = mybir.dt.float32
BF16 = mybir.dt.bfloat16
I32 = mybir.dt.int32
AF = mybir.ActivationFunctionType
ALU = mybir.AluOpType

P = 128
NTP = 512
NS = 64
SEQ = 4096
D = 1024
DFF = 2816
KC = 8
EPS = 1e-6
NEG = -30000.0
MLA_SCALE = float((128 + 64) ** -0.5)
GAMMAS = [1.0 - 2.0 ** (-5.0 - h) for h in range(4)]
N_CORES = 8
PAST = 8192


class View:
    __slots__ = ("ap", "pages")

    def __init__(self, ap, pages):
        self.ap = ap
        self.pages = pages

    def sub(self, ap):
        return View(ap, self.pages)


class Sched:
    def __init__(self, nc):
        self.nc = nc
        self.eng = dict(pe=nc.tensor, act=nc.scalar, dve=nc.vector, pool=nc.gpsimd, sp=nc.sync)
        self.esem = {e: nc.alloc_semaphore("es_" + e) for e in self.eng}
        self.ecnt = {e: 0 for e in self.eng}
        self.seen = {e: {} for e in self.eng}
        self.lastw = {}
        self.readers = {}
        self.dcnt = {}
        self.dry = False
        self.nops = 0

    def op(self, e, fn, reads=(), writes=(), dma=None):
        if self.dry:
            return
        need = {}

        def want(sig, raw):
            if sig is None:
                return
            sem, val, se = sig
            if se == e and (not raw or e == "pe"):
                return
            k = id(sem)
            if self.seen[e].get(k, 0) >= val:
                return
            if k not in need or need[k][1] < val:
                need[k] = (sem, val)

        for v in reads:
            for p in v.pages:
                want(self.lastw.get(p), True)
        for v in writes:
            for p in v.pages:
                want(self.lastw.get(p), False)
                rd = self.readers.get(p)
                if rd:
                    for sg in rd.values():
                        want(sg, False)
        eng = self.eng[e]
        for k, (sem, val) in need.items():
            eng.wait_ge(sem, val)
            self.seen[e][k] = val
        ins = fn(eng)
        self.nops += 1
        if dma is not None:
            c = self.dcnt.get(id(dma), 0) + 16
            self.dcnt[id(dma)] = c
            ins.then_inc(dma, 16)
            sig = (dma, c, None)
        else:
            self.ecnt[e] += 1
            ins.then_inc(self.esem[e], 1)
            sig = (self.esem[e], self.ecnt[e], e)
        for v in writes:
            for p in v.pages:
                self.lastw[p] = sig
                self.readers[p] = {}
        k = id(sig[0])
        for v in reads:
            for p in v.pages:
                self.readers.setdefault(p, {})[k] = sig
        return sig


class Buf:
    def __init__(self, t, off, dtype, n):
        self.t = t
        self.off = off
        self.sz = 4 if dtype in (F32, I32) else 2
        self.n = n

    def pages(self, lo, n):
        b0 = self.off + lo * self.sz
        b1 = self.off + (lo + n) * self.sz
        return [("S", i) for i in range(b0 // 1024, (b1 - 1) // 1024 + 1)]

    def v(self, lo=0, n=None, parts=None):
        if n is None:
            n = self.n - lo
        ap = self.t[:, lo:lo + n] if parts is None else self.t[parts[0]:parts[1], lo:lo + n]
        return View(ap, self.pages(lo, n))

    def v3(self, lo, c, w, parts=None):
        ap = self.t[:, lo:lo + c * w].rearrange("p (c w) -> p c w", c=c)
        if parts is not None:
            ap = ap[parts[0]:parts[1]]
        return View(ap, self.pages(lo, c * w))


class Mem:
    def __init__(self, nc, base, top):
        self.nc = nc
        self.off = base
        self.top = top
        self.k = 0

    def alloc(self, name, n, dtype, at=None, align=1024):
        sz = 4 if dtype in (F32, I32) else 2
        if at is None:
            at = (self.off + align - 1) // align * align
            self.off = at + n * sz
            assert self.off <= self.top, (name, self.off, self.top)
        self.k += 1
        t = self.nc.alloc_sbuf_tensor_at(f"{name}_{self.k}", [P, n], dtype, offset=at)
        return Buf(t, at, dtype, n)


class WStream:
    def __init__(self, sch, mem, nslots, slot_elems):
        self.sch = sch
        self.R = nslots
        self.slots = [mem.alloc(f"wslot{i}", slot_elems, BF16) for i in range(nslots)]
        self.sems = [sch.nc.alloc_semaphore(f"wsem{i}") for i in range(nslots)]
        self.plan = []
        self.cur = 0
        self.issued = 0

    def get(self, dram_ap, n, hold=0):
        if self.sch.dry:
            self.plan.append((dram_ap, n))
            i = len(self.plan) - 1
            return self.slots[i % self.R]
        assert self.plan[self.cur][1] == n
        while self.issued < min(self.cur + self.R - hold, len(self.plan)):
            self._issue(self.issued)
            self.issued += 1
        b = self.slots[self.cur % self.R]
        self.cur += 1
        return b

    def _issue(self, i):
        ap, n = self.plan[i]
        b = self.slots[i % self.R]
        self.sch.op("pool", lambda e: e.dma_start(out=b.t[:, 0:n], in_=ap),
                    writes=[b.v(0, n)], dma=self.sems[i % self.R])


def _unit(Wcols):
    K, n = Wcols.shape
    kc = K // P
    return np.ascontiguousarray(Wcols.reshape(kc, P, n).transpose(1, 0, 2).reshape(P, kc * n))


def _fm(x):
    T, F = x.shape
    return np.ascontiguousarray(x.T.reshape(F // P, P, T).transpose(1, 0, 2))


def _unfm(a):
    p, c, t = a.shape
    return np.ascontiguousarray(a.transpose(2, 1, 0).reshape(t, c * p))


PV_SPEC = ([(f"norm_mix{i}", 8) for i in range(4)] + [(f"norm_ffn{i}", 8) for i in range(4)] + [("norm_final", 8)]
           + [(f"{n}{j}", c) for j in range(2) for n, c in
              (("conv_b_pw1", 16), ("conv_b_dw", 8), ("conv_ln_g", 8), ("conv_ln_b", 8), ("conv_b_pw2", 8))]
           + [("mla_g_q", 3), ("mla_g_kv", 2), ("ret_gn_g", 16), ("ret_gn_b", 16), ("onehot", 16)])
PV_IDX = {}
_c = 0
for _n, _k in PV_SPEC:
    PV_IDX[_n] = _c
    _c += _k
PV_COLS = _c

CM_ONES, CM_ID, CM_M1024, CM_M384, CM_M256, CM_M512 = 0, 128, 256, 384, 512, 640
CM_MASKNEG = 768
CM_MASK01 = CM_MASKNEG + 2048
CM_GQ = CM_MASK01 + 2048
CM_GK = CM_GQ + 2048
CM_CROSS = CM_GK + 2048
CM_NEWM = CM_CROSS + 64
CM_GQS = CM_NEWM + 512
CM_GKS = CM_GQS + 256
CM_M01S = CM_GKS + 256
CM_COLS = CM_M01S + 64


def host_consts():
    cm = np.zeros((P, CM_COLS), np.float32)
    cm[:, CM_ONES:CM_ONES + 128] = 1.0
    cm[:, CM_ID:CM_ID + 128] = np.eye(128, dtype=np.float32)
    cm[:, CM_M1024:CM_M1024 + 128] = 1.0 / 1024
    cm[:, CM_M384:CM_M384 + 128] = 1.0 / 384
    cm[:, CM_M256:CM_M256 + 128] = 1.0 / 256
    cm[:, CM_M512:CM_M512 + 128] = 1.0 / 512
    s = np.arange(P)[:, None]
    q = np.arange(512)[None, :]
    for d in range(4):
        ok = q >= d * 128 + s
        cm[:, CM_MASKNEG + d * 512:CM_MASKNEG + (d + 1) * 512] = np.where(ok, 0.0, NEG)
        cm[:, CM_MASK01 + d * 512:CM_MASK01 + (d + 1) * 512] = np.where(ok, 1.0, 0.0)
    n = np.arange(512, dtype=np.float64)
    for h in range(4):
        g = GAMMAS[h]
        cm[:, CM_GQ + h * 512:CM_GQ + (h + 1) * 512] = (g ** n)[None, :]
        cm[:, CM_GK + h * 512:CM_GK + (h + 1) * 512] = (g ** (-n) / 16.0)[None, :]
    tq = (np.arange(64) % 4).astype(np.float64)
    for h in range(4):
        g = GAMMAS[h]
        cm[:, CM_GQS + h * 64:CM_GQS + (h + 1) * 64] = (g ** tq)[None, :]
        cm[:, CM_GKS + h * 64:CM_GKS + (h + 1) * 64] = (g ** (-tq) / 16.0)[None, :]
    mm_ = np.arange(64)[:, None]
    nn_ = np.arange(64)[None, :]
    cm[:64, CM_M01S:CM_M01S + 64] = np.where(((mm_ // 4) == (nn_ // 4)) & (mm_ <= nn_), 1.0, 0.0)
    col = np.arange(64)
    pp = np.arange(P)[:, None]
    cm[:, CM_CROSS:CM_CROSS + 64] = np.where((pp // 64) == ((col[None, :] % 8) // 4), 0.0, NEG)
    kk = np.arange(64)[:, None]
    for bp in range(8):
        ok = ((kk // 4) == 2 * bp + (col[None, :] % 8) // 4) & ((kk % 4) <= (col[None, :] % 4))
        cm[:64, CM_NEWM + bp * 64:CM_NEWM + (bp + 1) * 64] = np.where(ok, 0.0, NEG)
    pos = np.concatenate([np.arange(SEQ), np.tile(PAST + np.arange(4), 16)]).astype(np.float32)
    inv_m = (10000.0 ** (-np.arange(0, 64, 2, dtype=np.float32) / 64)).astype(np.float32)
    ang = (pos[None, :] * inv_m[:, None]).astype(np.float32)
    cosm = np.concatenate([np.cos(ang), np.cos(ang)], 0).astype(np.float32)
    sinm = np.concatenate([-np.sin(ang), np.sin(ang)], 0).astype(np.float32)
    inv_r = (1.0 / (10000.0 ** np.linspace(0.0, 1.0, 128, dtype=np.float32))).astype(np.float32)
    angr = (pos[None, :] * inv_r[:, None]).astype(np.float32)
    ropem = np.zeros((P, 2, SEQ + NS), np.float32)
    ropem[:64, 0] = cosm
    ropem[:64, 1] = sinm
    roper = np.stack([np.cos(angr), np.sin(angr)], 1).astype(np.float32)
    return cm, ropem, roper


def host_weights(inp):
    w = {}
    A = lambda k: np.asarray(inp[k], np.float32)
    pv = np.zeros((P, PV_COLS), np.float32)

    def put(name, vec):
        c0 = PV_IDX[name]
        k = vec.shape[0] // P
        pv[:, c0:c0 + k] = vec.reshape(k, P).T

    for i in range(4):
        put(f"norm_mix{i}", A("norm_mix")[i])
        put(f"norm_ffn{i}", A("norm_ffn")[i])
    put("norm_final", A("norm_final"))
    for j in range(2):
        for n in ("conv_b_pw1", "conv_b_dw", "conv_ln_g", "conv_ln_b", "conv_b_pw2"):
            put(f"{n}{j}", A(n)[j])
    put("mla_g_q", A("mla_g_q")[0])
    put("mla_g_kv", A("mla_g_kv")[0])
    put("ret_gn_g", A("ret_gn_g")[0])
    put("ret_gn_b", A("ret_gn_b")[0])
    c0 = PV_IDX["onehot"]
    for bi in range(16):
        pv[4 * bi:4 * bi + 4, c0 + bi] = 1.0
    w["pv"] = pv
    w13 = np.zeros((4, 11, P, 4096), np.float32)
    w2 = np.zeros((4, 8, P, 2816), np.float32)
    for i in range(4):
        W1, W3, W2 = A("ffn_w1")[i], A("ffn_w3")[i], A("ffn_w2")[i]
        for u in range(11):
            cols = []
            for f in (2 * u, 2 * u + 1):
                cols += [W1[:, f * 128:(f + 1) * 128], W3[:, f * 128:(f + 1) * 128]]
            w13[i, u] = _unit(np.concatenate(cols, 1))
        for dc in range(8):
            w2[i, dc] = _unit(W2[:, dc * 128:(dc + 1) * 128])
    w["w13"], w["w2"] = w13, w2
    cpw1 = np.zeros((2, 4, P, 4096), np.float32)
    cdw = np.zeros((2, 8, P, 31 * 128), np.float32)
    cpw2 = np.zeros((2, 2, P, 4096), np.float32)
    for j in range(2):
        W = A("conv_w_pw1")[j]
        for u in range(4):
            cols = []
            for oc in (2 * u, 2 * u + 1):
                cols += [W[:, oc * 128:(oc + 1) * 128], W[:, 1024 + oc * 128:1024 + (oc + 1) * 128]]
            cpw1[j, u] = _unit(np.concatenate(cols, 1))
        wd = A("conv_w_dw")[j]
        for oc in range(8):
            dg = np.zeros((P, 31, P), np.float32)
            idx = np.arange(P)
            dg[idx, :, idx] = wd[:, oc * 128:(oc + 1) * 128].T
            cdw[j, oc] = dg.reshape(P, 31 * 128)
        W = A("conv_w_pw2")[j]
        for u in range(2):
            cpw2[j, u] = _unit(W[:, u * 512:(u + 1) * 512])
    w["cpw1"], w["cdw"], w["cpw2"] = cpw1, cdw, cpw2
    Wuq = A("mla_w_uq")[0].reshape(384, 8, 192)
    nope = Wuq[:, :, :128].reshape(384, 1024)
    ra = Wuq[:, :, 128:192].reshape(384, 512)
    rb = np.concatenate([Wuq[:, :, 160:192], Wuq[:, :, 128:160]], 2).reshape(384, 512)
    Wdkv = A("mla_w_dkv")[0]
    dkv = np.concatenate([Wdkv[:, :256], Wdkv[:, 256:320], Wdkv[:, 288:320], Wdkv[:, 256:288]], 1)
    Wuk = A("mla_w_uk")[0]
    uk = np.ascontiguousarray(Wuk.transpose(2, 1, 0).reshape(P, 8 * 256))
    Wuv = A("mla_w_uv")[0]
    uv = np.ascontiguousarray(Wuv.reshape(2, P, 8, 128).transpose(1, 2, 0, 3).reshape(P, 8 * 256))
    Wo = A("mla_w_o")[0]
    w["m_dq"] = _unit(A("mla_w_dq")[0])[None]
    w["m_uq"] = np.stack([_unit(nope), _unit(np.concatenate([ra, rb], 1))])
    w["m_uk"] = uk[None]
    w["m_dkv"] = _unit(dkv)[None]
    w["m_uv"] = uv[None]
    w["m_o"] = np.stack([_unit(Wo[:, u * 512:(u + 1) * 512]) for u in range(2)])
    w["r_q"] = np.stack([_unit(A("ret_w_q")[0][:, u * 512:(u + 1) * 512]) for u in range(2)])
    w["r_k"] = np.stack([_unit(A("ret_w_k")[0][:, u * 512:(u + 1) * 512]) for u in range(2)])
    w["r_v"] = np.stack([_unit(A("ret_w_v")[0][:, u * 512:(u + 1) * 512]) for u in range(4)])
    w["r_g"] = np.stack([_unit(A("ret_w_g")[0][:, u * 512:(u + 1) * 512]) for u in range(4)])
    w["r_o"] = np.stack([_unit(A("ret_w_o")[0][:, u * 256:(u + 1) * 256]) for u in range(4)])
    return w


W_SHAPES = {
    "pv": (P, PV_COLS), "w13": (4, 11, P, 4096), "w2": (4, 8, P, 2816),
    "cpw1": (2, 4, P, 4096), "cdw": (2, 8, P, 3968), "cpw2": (2, 2, P, 4096),
    "m_dq": (1, P, 3072), "m_uq": (2, P, 3072), "m_uk": (1, P, 2048), "m_dkv": (1, P, 3072),
    "m_uv": (1, P, 2048), "m_o": (2, P, 4096),
    "r_q": (2, P, 4096), "r_k": (2, P, 4096), "r_v": (4, P, 4096), "r_g": (4, P, 4096), "r_o": (4, P, 4096),
}


class Prog:
    def __init__(self, n_tiles=8, depth=4, sample=True, debug=False):
        self.debug = debug
        self.n_tiles, self.depth, self.sample = n_tiles, depth, sample
        nc = bass.Bass("TRN2", target_bir_lowering=False)
        self.nc = nc
        self.sch = Sched(nc)
        d = {}
        for k, shp in W_SHAPES.items():
            d[k] = nc.dram_tensor(k, list(shp), F32, kind="ExternalInput").ap()
        d["cm"] = nc.dram_tensor("cm", [P, CM_COLS], F32, kind="ExternalInput").ap()
        d["ropem"] = nc.dram_tensor("ropem", [P, 2, SEQ + NS], F32, kind="ExternalInput").ap()
        d["roper"] = nc.dram_tensor("roper", [P, 2, SEQ + NS], F32, kind="ExternalInput").ap()
        d["xp"] = nc.dram_tensor("xp", [P, KC, SEQ], F32, kind="ExternalInput").ap()
        d["xs"] = nc.dram_tensor("xs", [P, KC, NS], F32, kind="ExternalInput").ap()
        d["sconv"] = nc.dram_tensor("sconv", [2, P, KC, 16 * 30], F32, kind="ExternalInput").ap()
        d["clat"] = nc.dram_tensor("clat", [10240 * 16, 2048], F32, kind="ExternalInput").ap()
        d["ckr"] = nc.dram_tensor("ckr", [10240 * 16, 512], F32, kind="ExternalInput").ap()
        d["pt"] = nc.dram_tensor("pt", [P, 8], I32, kind="ExternalInput").ap()
        d["rsin"] = nc.dram_tensor("rsin", [16, P, 8, 512], F32, kind="ExternalInput").ap()
        O = lambda n, s: nc.dram_tensor(n, s, F32, kind="ExternalOutput").ap()
        d["yp"] = O("yp", [P, KC, SEQ])
        d["ys"] = O("ys", [P, KC, NS])
        d["cvp"] = O("cvp", [2, P, KC, 30])
        d["cvs"] = O("cvs", [2, P, KC, 16 * 30])
        d["latp"] = O("latp", [P, 2, SEQ])
        d["krp"] = O("krp", [64, SEQ])
        d["lats"] = O("lats", [P, 2, NS])
        d["krs"] = O("krs", [64, NS])
        d["rsp"] = O("rsp", [P, 8, 512])
        d["rss"] = O("rss", [16, P, 8, 512])
        if debug:
            d["dbg"] = O("dbg", [P, 16 * 512])
            d["dbg2"] = O("dbg2", [P, 16 * 512])
            d["dbg3"] = O("dbg3", [P, 8 * 512])
            d["dbg4"] = O("dbg4", [P, 1024])
        self.d = d
        self.build()

    def op(self, e, fn, r=(), w=(), dma=None):
        return self.sch.op(e, fn, r, w, dma)

    def pv(self, name, c):
        i = PV_IDX[name] + c
        return self.pvb.v(i, 1)

    def cmv(self, col, n=128, parts=None):
        return self.cmb.v(col, n, parts)

    def bank(self, b, lo=0, n=512, parts=None):
        ap = self.ps[:, b * 512 + lo:b * 512 + lo + n] if parts is None else self.ps[parts[0]:parts[1], b * 512 + lo:b * 512 + lo + n]
        return View(ap, [("P", b)])

    def mm(self, out, pairs, start=True, stop=True):
        n = len(pairs)

        def fn(e):
            ins = None
            for i, (l, r) in enumerate(pairs):
                ins = e.matmul(out.ap, lhsT=l.ap, rhs=r.ap, start=(start and i == 0), stop=(stop and i == n - 1))
            return ins
        self.op("pe", fn, r=[x for pr in pairs for x in pr], w=[out])

    def pstv(self, lo, n, parts=None):
        ap = self.pst[:, lo:lo + n] if parts is None else self.pst[parts[0]:parts[1], lo:lo + n]
        return View(ap, [("P", 6 + lo // 1024)] if (lo + n - 1) // 1024 == lo // 1024 else [("P", 6), ("P", 7)])

    def tr(self, out, src, kparts=128):
        idv = self.cmv(CM_ID, kparts, parts=(0, kparts))
        self.op("pe", lambda e: e.transpose(out=out.ap, in_=src.ap, identity=idv.ap), r=[src, idv], w=[out])

    def acopy(self, out, src, scale=None):
        if scale is None:
            self.op("act", lambda e: e.activation(out=out.ap, in_=src.ap, func=AF.Copy), r=[src], w=[out])
        else:
            self.op("act", lambda e: e.activation(out=out.ap, in_=src.ap, func=AF.Copy, scale=scale), r=[src], w=[out])

    def vcopy(self, out, src):
        self.op("dve", lambda e: e.tensor_copy(out=out.ap, in_=src.ap), r=[src], w=[out])

    def vtt(self, out, a, b, op):
        self.op("dve", lambda e: e.tensor_tensor(out=out.ap, in0=a.ap, in1=b.ap, op=op), r=[a, b], w=[out])

    def build(self):
        nc, sch, d = self.nc, self.sch, self.d
        mem = Mem(nc, 16512, nc.sbuf_top)
        self.mem = mem
        self.ps = nc.alloc_psum_tensor("ps", [P, 6 * 512], F32)
        self.pst = nc.alloc_psum_tensor("pst", [P, 2048], BF16)
        self.cmb = mem.alloc("cm", CM_COLS, BF16)
        self.pvb = mem.alloc("pv", PV_COLS, F32, align=64)
        self.ptb = mem.alloc("ptb", 8, I32, align=64)
        self.idxb = mem.alloc("idxb", 128, I32, align=64)
        self.epsb = mem.alloc("eps", 8, F32, align=64)
        self.epsv = self.epsb.v(0, 1)
        self.xT = mem.alloc("xT", KC * NTP, F32)
        self.hT = mem.alloc("hT", KC * NTP, BF16)
        self.ub = [mem.alloc(f"ub{j}", KC * (30 + NTP), BF16) for j in range(2)]
        self.ws = WStream(sch, mem, 3, 4096)
        self.cT = mem.alloc("cT", 2 * (SEQ + NS), BF16)
        self.ctok = mem.alloc("ctok", 33 * 256, BF16)
        self.krT = mem.alloc("krT", SEQ + NS, BF16)
        self.Sst = mem.alloc("Sst", 8 * 512, F32)
        self.scr0 = (mem.off + 1023) // 1024 * 1024
        self.scr_sz = mem.top - self.scr0
        for dry in (True, False):
            sch.dry = dry
            self.k_scr = 0
            self.run()
        for s in self._dsems.values():
            c = sch.dcnt.get(id(s), 0)
            if c:
                nc.sync.wait_ge(s, c)

    def scratch(self, name, n, dtype, at):
        sz = 4 if dtype in (F32, I32) else 2
        assert at % 64 == 0 and at + n * sz <= self.scr_sz, (name, at, n, self.scr_sz)
        key = (name, n, dtype, at)
        if not hasattr(self, "_scache"):
            self._scache = {}
        if key not in self._scache:
            self._scache[key] = self.mem.alloc(name, n, dtype, at=self.scr0 + at)
        return self._scache[key]

    def dsem(self, name):
        if not hasattr(self, "_dsems"):
            self._dsems = {}
        if name not in self._dsems:
            self._dsems[name] = self.nc.alloc_semaphore("ds_" + name)
        return self._dsems[name]

    def arena(self, base=0):
        prog = self

        class A:
            def __init__(a):
                a.off = base

            def take(a, name, n, dtype):
                sz = 4 if dtype in (F32, I32) else 2
                at = (a.off + 1023) // 1024 * 1024
                a.off = at + n * sz
                return prog.scratch(f"{name}_{n}", n, dtype, at)
        return A()

    def run(self):
        d = self.d
        if not self.sch.dry:
            self.op("pool", lambda e: e.dma_start(out=self.cmb.t[:, 0:CM_COLS // 2], in_=d["cm"][:, 0:CM_COLS // 2]),
                    w=[self.cmb.v(0, CM_COLS // 2)], dma=self.dsem("c0"))
            self.op("pool", lambda e: e.dma_start(out=self.cmb.t[:, CM_COLS // 2:], in_=d["cm"][:, CM_COLS // 2:]),
                    w=[self.cmb.v(CM_COLS // 2)], dma=self.dsem("c1"))
            self.op("sp", lambda e: e.dma_start(out=self.pvb.t[:, :], in_=d["pv"]), w=[self.pvb.v()], dma=self.dsem("c2"))
            self.op("dve", lambda e: e.memset(self.epsb.t[:, :], EPS), w=[self.epsb.v()])
            self.op("sp", lambda e: e.dma_start(out=self.ptb.t[:, :], in_=d["pt"]), w=[self.ptb.v()], dma=self.dsem("c3"))
            iv = View(self.idxb.t[:, :].rearrange("p (b g) -> p b g", g=16), self.idxb.pages(0, 128))
            for g in range(16):
                self.op("dve", lambda e, g=g: e.tensor_scalar(out=iv.ap[:, :, g], in0=self.ptb.t[:, :], scalar1=16, scalar2=g,
                                                              op0=ALU.mult, op1=ALU.add), r=[self.ptb.v()], w=[iv])
            for j in range(2):
                self.op("dve", lambda e, j=j: e.memset(self.ub[j].t[:, :], 0.0), w=[self.ub[j].v()])
            self.op("dve", lambda e: e.memset(self.Sst.t[:, :], 0.0), w=[self.Sst.v()])
        tiles = [("p", i) for i in range(self.n_tiles)]
        if self.sample:
            tiles.append(("s", 0))
        for kind, ti in tiles:
            self.tile(kind, ti)

    def tile(self, kind, ti):
        d = self.d
        NT = NTP if kind == "p" else NS
        self.NT, self.kind, self.ti = NT, kind, ti
        t0 = ti * NTP
        xv = self.xT.v3(0, KC, NTP)
        src = d["xp"][:, :, t0:t0 + NT] if kind == "p" else d["xs"]
        self.op("sp", lambda e: e.dma_start(out=xv.ap[:, :, 0:NT], in_=src), w=[xv], dma=self.dsem("x"))
        for li in range(self.depth):
            m = li % 3
            self.rmsnorm(f"norm_mix{li}")
            if m == 0:
                self.conv_layer(li // 3)
            elif m == 1:
                self.mla_layer()
            else:
                self.ret_layer()
            self.rmsnorm(f"norm_ffn{li}")
            self.ffn(li)
        self.final_norm()

    def xc(self, c):
        return self.xT.v(c * NTP, self.NT)

    def hc(self, c):
        return self.hT.v(c * NTP, self.NT)

    def stats(self, src_chunks, mcol, outbank, nparts=None):
        NT = self.NT
        out = self.bank(outbank, 0, NT)
        self.mm(out, [(self.cmv(mcol), s) for s in src_chunks])
        return out

    def rsqrt(self, out, src, recip=True):
        self.op("act", lambda e: e.activation(out=out.ap, in_=src.ap, func=AF.Ln, bias=self.epsv.ap), r=[src, self.epsv], w=[out])
        self.op("act", lambda e: e.activation(out=out.ap, in_=out.ap, func=AF.Exp, scale=-0.5), r=[out], w=[out])

    def rmsnorm(self, gname, dst=None):
        NT = self.NT
        sq = self.scratch("sq", KC * NTP, BF16, 0)
        rstd = self.scratch("rstd", NTP, F32, KC * NTP * 2)
        for c in range(KC):
            x = self.xc(c)
            o = sq.v(c * NTP, NT)
            self.op("act", lambda e, x=x, o=o: e.activation(out=o.ap, in_=x.ap, func=AF.Square), r=[x], w=[o])
        st = self.stats([sq.v(c * NTP, NT) for c in range(KC)], CM_M1024, 5)
        rv = rstd.v(0, NT)
        self.rsqrt(rv, st)
        for c in range(KC):
            x = self.xc(c)
            g = self.pv(gname, c)
            o = self.hc(c) if dst is None else dst(c)
            self.op("dve", lambda e, x=x, g=g, o=o: e.scalar_tensor_tensor(out=o.ap, in0=x.ap, scalar=g.ap, in1=rv.ap,
                                                                            op0=ALU.mult, op1=ALU.mult), r=[x, g, rv], w=[o])

    def final_norm(self):
        NT = self.NT
        d = self.d
        yb = self.scratch("ybuf", KC * NTP, F32, 20480)
        self.rmsnorm("norm_final", dst=lambda c: yb.v(c * NTP, NT))
        yv = yb.v3(0, KC, NTP)
        if self.kind == "p":
            t0 = self.ti * NTP
            dst = d["yp"][:, :, t0:t0 + NT]
        else:
            dst = d["ys"]
        self.op("sp", lambda e: e.dma_start(out=dst, in_=yv.ap[:, :, 0:NT]), r=[yv], dma=self.dsem("y"))

    def ffn(self, li):
        NT = self.NT
        d = self.d
        g = self.scratch("ffn_g", 22 * NTP, BF16, 0)
        sa = [self.scratch(f"ffn_sa{i}", NTP, BF16, 22 * NTP * 2 + i * 1024) for i in range(2)]
        bk = 0
        for u in range(11):
            wb = self.ws.get(d["w13"][li, u], 4096)
            for f2 in range(2):
                fc = 2 * u + f2
                pa, pb = self.bank(bk % 4, 0, NT), self.bank((bk + 1) % 4, 0, NT)
                bk += 2
                for which, pso in ((0, pa), (1, pb)):
                    co = (2 * f2 + which) * 128
                    self.mm(pso, [(wb.v(k * 512 + co, 128), self.hc(k)) for k in range(KC)])
                s = sa[fc % 2].v(0, NT)
                self.op("act", lambda e, s=s, pa=pa: e.activation(out=s.ap, in_=pa.ap, func=AF.Silu), r=[pa], w=[s])
                go = g.v(fc * NTP, NT)
                self.op("dve", lambda e, s=s, pb=pb, go=go: e.tensor_tensor(out=go.ap, in0=s.ap, in1=pb.ap, op=ALU.mult),
                        r=[s, pb], w=[go])
        for dc in range(8):
            wb = self.ws.get(d["w2"][li, dc], 2816)
            po = self.bank(bk % 4, 0, NT)
            bk += 1
            self.mm(po, [(wb.v(k * 128, 128), g.v(k * NTP, NT)) for k in range(22)])
            x = self.xc(dc)
            self.op("dve", lambda e, x=x, po=po: e.tensor_tensor(out=x.ap, in0=x.ap, in1=po.ap, op=ALU.add), r=[x, po], w=[x])

    def conv_layer(self, j):
        NT = self.NT
        d = self.d
        kind = self.kind
        ub = self.ub[j]
        W = 30 + NTP
        CW = NT
        ar = self.arena(0)
        if kind == "s":
            ubs = ar.take("ubs", KC * 16 * 34, BF16)
            ubf = ar.take("ubf", KC * 16 * 34, F32)
            dstv = View(ubf.t[:, :].rearrange("p (c b r) -> p c b r", c=KC, b=16), ubf.pages(0, ubf.n))
            self.op("sp", lambda e: e.dma_start(
                out=dstv.ap[:, :, :, 0:30], in_=d["sconv"][j].rearrange("p c (b r) -> p c b r", b=16)),
                w=[dstv], dma=self.dsem(f"sconv{j}"))
        cf = ar.take("cv_c", KC * CW, F32)
        cb = ar.take("cv_cb", KC * CW, BF16)
        cq = ar.take("cv_cq", KC * CW, BF16)
        sg = [ar.take(f"cv_sg{i}", CW, F32) for i in range(2)]
        tt = [ar.take(f"cv_t{i}", CW, F32) for i in range(2)]
        stg = ar.take("cv_stg", KC * 30, F32)
        mub = ar.take("cv_mu", CW, F32)
        rsb = ar.take("cv_rs", CW, F32)
        bk = 0
        for u in range(4):
            wb = self.ws.get(d["cpw1"][j, u], 4096)
            for o2 in range(2):
                oc = 2 * u + o2
                pa, pb = self.bank(bk % 4, 0, NT), self.bank((bk + 1) % 4, 0, NT)
                bk += 2
                for which, pso in ((0, pa), (1, pb)):
                    co = (2 * o2 + which) * 128
                    self.mm(pso, [(wb.v(k * 512 + co, 128), self.hc(k)) for k in range(KC)])
                s = sg[oc % 2].v(0, NT)
                bg = self.pv(f"conv_b_pw1{j}", 8 + oc)
                bv = self.pv(f"conv_b_pw1{j}", oc)
                self.op("act", lambda e, s=s, pb=pb, bg=bg: e.activation(out=s.ap, in_=pb.ap, func=AF.Sigmoid, bias=bg.ap),
                        r=[pb, bg], w=[s])
                if kind == "p":
                    uo = ub.v(oc * W + 30, NT)
                    self.op("dve", lambda e, uo=uo, pa=pa, bv=bv, s=s: e.scalar_tensor_tensor(
                        out=uo.ap, in0=pa.ap, scalar=bv.ap, in1=s.ap, op0=ALU.add, op1=ALU.mult), r=[pa, bv, s], w=[uo])
                else:
                    uo = ubf.v3(oc * 544, 16, 34)
                    self.op("dve", lambda e, uo=uo, pa=pa, bv=bv, s=s: e.scalar_tensor_tensor(
                        out=uo.ap[:, :, 30:34], in0=pa.ap.rearrange("p (b t) -> p b t", b=16), scalar=bv.ap,
                        in1=s.ap.rearrange("p (b t) -> p b t", b=16), op0=ALU.add, op1=ALU.mult), r=[pa, bv, s], w=[uo])
                    ubv = ubs.v(oc * 544, 544)
                    uof = ubf.v(oc * 544, 544)
                    self.op("act", lambda e, ubv=ubv, uof=uof: e.activation(out=ubv.ap, in_=uof.ap, func=AF.Copy), r=[uof], w=[ubv])
        if kind == "s":
            sv = View(ubf.t[:, :].rearrange("p (c b r) -> p c b r", c=KC, b=16), ubf.pages(0, ubf.n))
            self.op("sp", lambda e: e.dma_start(
                out=d["cvs"][j].rearrange("p c (b r) -> p c b r", b=16), in_=sv.ap[:, :, :, 4:34]),
                r=[sv], dma=self.dsem(f"cvs{j}"))
        for oc in range(KC):
            wb = self.ws.get(d["cdw"][j, oc], 3968)
            pc = self.bank(bk % 4, 0, NT)
            bk += 1
            if kind == "p":
                src = ub.v(oc * W, W)
                pairs = [(wb.v(t * 128, 128), src.sub(src.ap[:, t:t + NT])) for t in range(31)]
                self.mm(pc, pairs)
            else:
                src = ubs.v3(oc * 544, 16, 34)
                n = 31
                pco = pc.sub(pc.ap.rearrange("p (b t) -> p b t", b=16))

                def fn(e, wb=wb, src=src, pco=pco):
                    ins = None
                    for t in range(31):
                        ins = e.matmul(pco.ap, lhsT=wb.t[:, t * 128:(t + 1) * 128], rhs=src.ap[:, :, t:t + 4],
                                       start=(t == 0), stop=(t == 30))
                    return ins
                self.op("pe", fn, r=[wb.v(0, 3968), src], w=[pc])
            bd = self.pv(f"conv_b_dw{j}", oc)
            co, cbo, cqo = cf.v(oc * CW, NT), cb.v(oc * CW, NT), cq.v(oc * CW, NT)
            self.op("act", lambda e, co=co, pc=pc, bd=bd: e.activation(out=co.ap, in_=pc.ap, func=AF.Identity, bias=bd.ap),
                    r=[pc, bd], w=[co])
            self.op("act", lambda e, co=co, cbo=cbo: e.activation(out=cbo.ap, in_=co.ap, func=AF.Copy), r=[co], w=[cbo])
            self.op("act", lambda e, co=co, cqo=cqo: e.activation(out=cqo.ap, in_=co.ap, func=AF.Square), r=[co], w=[cqo])
        if kind == "p":
            if self.ti == SEQ // NTP - 1:
                hv = ub.v3(0, KC, W)
                sv = stg.v3(0, KC, 30)
                self.op("act", lambda e: e.activation(out=sv.ap, in_=hv.ap[:, :, NTP:NTP + 30], func=AF.Copy), r=[hv], w=[sv])
                self.op("sp", lambda e: e.dma_start(out=d["cvp"][j], in_=sv.ap), r=[sv], dma=self.dsem(f"cvp{j}"))
            hv = ub.v3(0, KC, W)
            self.op("act", lambda e: e.activation(out=hv.ap[:, :, 0:30], in_=hv.ap[:, :, NTP:NTP + 30], func=AF.Copy), r=[hv], w=[hv])
        mean = self.stats([cb.v(c * CW, NT) for c in range(KC)], CM_M1024, 5)
        msq = self.stats([cq.v(c * CW, NT) for c in range(KC)], CM_M1024, 4)
        mu = mub.v(0, NT)
        rs = rsb.v(0, NT)
        self.op("act", lambda e: e.activation(out=mu.ap, in_=mean.ap, func=AF.Copy), r=[mean], w=[mu])
        self.op("dve", lambda e: e.tensor_tensor(out=rs.ap, in0=mu.ap, in1=mu.ap, op=ALU.mult), r=[mu], w=[rs])
        self.op("dve", lambda e: e.tensor_tensor(out=rs.ap, in0=msq.ap, in1=rs.ap, op=ALU.subtract), r=[msq, rs], w=[rs])
        self.rsqrt(rs, rs)
        sact = cb
        for oc in range(KC):
            co = cf.v(oc * CW, NT)
            t = tt[oc % 2].v(0, NT)
            self.op("dve", lambda e, t=t, co=co: e.tensor_tensor(out=t.ap, in0=co.ap, in1=mu.ap, op=ALU.subtract), r=[co, mu], w=[t])
            self.op("dve", lambda e, t=t: e.tensor_tensor(out=t.ap, in0=t.ap, in1=rs.ap, op=ALU.mult), r=[t, rs], w=[t])
            so = sact.v(oc * CW, NT)
            lg, lb = self.pv(f"conv_ln_g{j}", oc), self.pv(f"conv_ln_b{j}", oc)
            self.op("act", lambda e, so=so, t=t, lg=lg, lb=lb: e.activation(out=so.ap, in_=t.ap, func=AF.Silu, scale=lg.ap, bias=lb.ap),
                    r=[t, lg, lb], w=[so])
        for u in range(2):
            wb = self.ws.get(d["cpw2"][j, u], 4096)
            for o4 in range(4):
                dc = 4 * u + o4
                po = self.bank(bk % 4, 0, NT)
                bk += 1
                self.mm(po, [(wb.v(k * 512 + o4 * 128, 128), sact.v(k * CW, NT)) for k in range(KC)])
                x = self.xc(dc)
                b2 = self.pv(f"conv_b_pw2{j}", dc)
                self.op("dve", lambda e, x=x, po=po, b2=b2: e.scalar_tensor_tensor(
                    out=x.ap, in0=po.ap, scalar=b2.ap, in1=x.ap, op0=ALU.add, op1=ALU.add), r=[po, b2, x], w=[x])

    def rope64(self, out, pa, pb, cos, sin, t1, t2):
        NT = self.NT
        a, b = t1.v(0, NT, parts=(0, 64)), t2.v(0, NT, parts=(0, 64))
        self.vtt(a, pa, cos, ALU.mult)
        self.vtt(b, pb, sin, ALU.mult)
        self.vtt(out, a, b, ALU.add)

    def mla_layer(self):
        NT, d, kind, ti = self.NT, self.d, self.kind, self.ti
        CW = NT
        L = SEQ + NS
        t0 = ti * NTP if kind == "p" else SEQ
        cT, krT, ctok = self.cT, self.krT, self.ctok
        ar = self.arena(0)
        qr = ar.take("m_qr", 8 * CW, BF16)
        ql = ar.take("m_ql", 16 * CW, BF16)
        e0 = ar.off
        cqf = ar.take("m_cqf", 3 * CW, F32)
        cqs = ar.take("m_cqs", 3 * CW, BF16)
        cqn = ar.take("m_cqn", 3 * CW, BF16)
        ropet = ar.take("m_rope", 2 * CW, F32)
        t1 = ar.take("m_t1", CW, F32)
        t2 = ar.take("m_t2", CW, F32)
        rsd = ar.take("m_rstd", CW, F32)
        qn = [ar.take(f"m_qn{i}", CW, BF16) for i in range(2)]
        clat = ar.take("m_clat", 2 * CW, F32)
        krf = ar.take("m_krf", CW, F32)
        ar2 = self.arena(e0)
        PT = [ar2.take(f"m_pt{i}", 256 if kind == "s" else CW, BF16) for i in range(2)]
        rec = ar2.take("m_rec", CW, F32)
        ob = ar2.take("m_ob", 8 * CW, BF16)
        bk = 0

        def nb(parts=None):
            nonlocal bk
            b = self.bank(bk % 3, 0, NT, parts)
            bk += 1
            return b
        rview = View(ropet.t[0:64, 0:2 * CW].rearrange("p (a w) -> p a w", a=2), ropet.pages(0, 2 * CW))
        self.op("sp", lambda e: e.dma_start(out=rview.ap[:, :, 0:NT], in_=d["ropem"][0:64, :, t0:t0 + NT]),
                w=[rview], dma=self.dsem("ropem"))
        cos = ropet.v(0, NT, parts=(0, 64))
        sin = ropet.v(CW, NT, parts=(0, 64))
        rv = rsd.v(0, NT)

        def lowrank(wb, ncol, noc, mcol, gname, dst_f32):
            for oc in range(noc):
                pa = nb()
                self.mm(pa, [(wb.v(k * ncol + oc * 128, 128), self.hc(k)) for k in range(KC)])
                self.acopy(cqf.v(oc * CW, NT), pa)
                o = cqs.v(oc * CW, NT)
                self.op("act", lambda e, o=o, pa=pa: e.activation(out=o.ap, in_=pa.ap, func=AF.Square), r=[pa], w=[o])
            st = self.stats([cqs.v(oc * CW, NT) for oc in range(noc)], mcol, 5)
            self.rsqrt(rv, st)
            for oc in range(noc):
                x, g, o = cqf.v(oc * CW, NT), self.pv(gname, oc), dst_f32(oc)
                self.op("dve", lambda e, x=x, g=g, o=o: e.scalar_tensor_tensor(out=o.ap, in0=x.ap, scalar=g.ap, in1=rv.ap,
                                                                                op0=ALU.mult, op1=ALU.mult), r=[x, g, rv], w=[o])
        wdq = self.ws.get(d["m_dq"][0], 3072)
        lowrank(wdq, 384, 3, CM_M384, "mla_g_q", lambda oc: cqn.v(oc * CW, NT))
        wq0 = self.ws.get(d["m_uq"][0], 3072)
        wuk = self.ws.get(d["m_uk"][0], 2048, hold=1)
        for h in range(8):
            pa = nb()
            self.mm(pa, [(wq0.v(kc * 1024 + h * 128, 128), cqn.v(kc * CW, NT)) for kc in range(3)])
            qv = qn[h % 2].v(0, NT)
            self.acopy(qv, pa)
            for rc in range(2):
                pb = nb()
                self.mm(pb, [(wuk.v(h * 256 + rc * 128, 128), qv)])
                self.vcopy(ql.v((rc * 8 + h) * CW, NT), pb)
        wq1 = self.ws.get(d["m_uq"][1], 3072)
        for h in range(8):
            pa, pb = nb((0, 64)), nb((0, 64))
            self.mm(pa, [(wq1.v(kc * 1024 + h * 64, 64), cqn.v(kc * CW, NT)) for kc in range(3)])
            self.mm(pb, [(wq1.v(kc * 1024 + 512 + h * 64, 64), cqn.v(kc * CW, NT)) for kc in range(3)])
            self.rope64(qr.v(h * CW, NT, parts=(0, 64)), pa, pb, cos, sin, t1, t2)
        wkv = self.ws.get(d["m_dkv"][0], 3072)
        lowrank(wkv, 384, 2, CM_M256, "mla_g_kv", lambda oc: clat.v(oc * CW, NT))
        for oc in range(2):
            self.acopy(cT.v(oc * L + t0, NT), clat.v(oc * CW, NT))
        clv = clat.v3(0, 2, CW)
        dst = d["latp"][:, :, t0:t0 + NT] if kind == "p" else d["lats"]
        self.op("sp", lambda e: e.dma_start(out=dst, in_=clv.ap[:, :, 0:NT]), r=[clv], dma=self.dsem("lat_o"))
        pa, pb = nb((0, 64)), nb((0, 64))
        self.mm(pa, [(wkv.v(k * 384 + 256, 64), self.hc(k)) for k in range(KC)])
        self.mm(pb, [(wkv.v(k * 384 + 320, 64), self.hc(k)) for k in range(KC)])
        kf = krf.v(0, NT, parts=(0, 64))
        self.rope64(kf, pa, pb, cos, sin, t1, t2)
        self.acopy(krT.v(t0, NT, parts=(0, 64)), kf)
        dst2 = d["krp"][:, t0:t0 + NT] if kind == "p" else d["krs"]
        self.op("sp", lambda e: e.dma_start(out=dst2, in_=kf.ap), r=[kf], dma=self.dsem("kr_o"))
        if kind == "p":
            for tb in range(4):
                for rc in range(2):
                    self.tr(self.pstv((tb * 2 + rc) * 128, 128), cT.v(rc * L + t0 + tb * 128, 128))
            self.acopy(ctok.v(ti * 4 * 256, 1024), self.pstv(0, 1024))
        else:
            for rc in range(2):
                self.tr(self.pstv(rc * 128, 128, parts=(0, 64)), cT.v(rc * L + SEQ, 64))
            self.acopy(ctok.v(32 * 256, 256, parts=(0, 64)), self.pstv(0, 256, parts=(0, 64)))
        ident = self.cmv(CM_ID, 128)
        ones = self.cmv(CM_ONES, 128)
        if kind == "p" and self.debug:
            qv_all = ql.v(0, 16 * CW)
            self.op("pool", lambda e: e.dma_start(out=d["dbg2"], in_=qv_all.ap), r=[qv_all], dma=self.dsem("dbg2"))
            qr_all = qr.v(0, 8 * CW)
            self.op("pool", lambda e: e.dma_start(out=d["dbg3"], in_=qr_all.ap), r=[qr_all], dma=self.dsem("dbg3"))
            ck_all = ctok.v(0, 1024)
            self.op("pool", lambda e: e.dma_start(out=d["dbg4"], in_=ck_all.ap), r=[ck_all], dma=self.dsem("dbg4"))

        def expo(o, i):
            self.op("act", lambda e: e.activation(out=o.ap, in_=i.ap, func=AF.Exp, scale=MLA_SCALE), r=[i], w=[o])
        if kind == "p":
            nkt = 4 * (ti + 1)
            PT3 = PT + [ar2.take("m_pt2", CW, BF16)]
            O = [self.bank(0, 0, NT), self.bank(1, 0, NT)]
            DN = self.bank(2, 0, NT)
            steps = [(h, kt) for h in range(8) for kt in range(nkt)]

            acc = ar2.take("m_acc", CW, F32)
            accb = ar2.take("m_accb", CW, BF16)

            def s_step(i):
                h, kt = steps[i]
                dg = kt - 4 * ti
                lo = max(dg, 0) * 128
                n = NT - lo
                S = self.bank(3 + i % 3, lo, n)
                pairs = [(cT.v(rc * L + kt * 128, 128), ql.v((rc * 8 + h) * CW + lo, n)) for rc in range(2)]
                pairs.append((krT.v(kt * 128, 128, parts=(0, 64)), qr.v(h * CW + lo, n, parts=(0, 64))))
                if dg >= 0:
                    pairs.append((ident, self.cmv(CM_MASKNEG + dg * 512 + lo, n)))
                self.mm(S, pairs)
                expo(PT3[i % 3].v(lo, n), S)

            def pv_step(i):
                h, kt = steps[i]
                dg = kt - 4 * ti
                lo = max(dg, 0) * 128
                n = NT - lo
                pt = PT3[i % 3].v(lo, n)
                first, last = kt == 0, kt == nkt - 1
                for rc in range(2):
                    Ov = self.bank(rc, lo, n)
                    self.mm(Ov, [(ctok.v(kt * 256 + rc * 128, 128), pt)], start=first, stop=last)
                av = acc.v(lo, n)
                if first:
                    self.vcopy(av, pt)
                else:
                    self.vtt(av, av, pt, ALU.add)
                if last:
                    self.acopy(accb.v(0, NT), acc.v(0, NT))
                    self.mm(DN, [(ones, accb.v(0, NT))])
                    rcv = rec.v(0, NT)
                    self.op("act", lambda e: e.activation(out=rcv.ap, in_=DN.ap, func=AF.Ln), r=[DN], w=[rcv])
                    self.op("act", lambda e: e.activation(out=rcv.ap, in_=rcv.ap, func=AF.Exp, scale=-1.0), r=[rcv], w=[rcv])
                    for rc in range(2):
                        self.vtt(ql.v((rc * 8 + h) * CW, NT), O[rc], rcv, ALU.mult)
            s_step(0)
            for i in range(len(steps)):
                if i + 1 < len(steps):
                    s_step(i + 1)
                pv_step(i)
        else:
            NB = 3
            G = [ar2.take(f"m_G{i}", 2048, BF16) for i in range(NB)]
            GR = [ar2.take(f"m_GR{i}", 512, BF16) for i in range(NB)]
            KT = [ar2.take(f"m_KT{i}", 1024, BF16) for i in range(NB)]
            KRT = [ar2.take(f"m_KRT{i}", 512, BF16) for i in range(NB)]
            PT3 = [ar2.take(f"m_pts{i}", 256, BF16) for i in range(NB)]
            ptv = self.idxb.v()
            cross = self.cmv(CM_CROSS, 64)
            O = [self.bank(0, 0, 64), self.bank(1, 0, 64)]
            DN = self.bank(2, 0, 64)

            def qviews(bp):
                Q = [View(ql.t[:, rc * 8 * CW:(rc * 8 + 8) * CW].rearrange("p (h w) -> p h w", h=8)[:, :, bp * 8:bp * 8 + 8],
                          ql.pages(rc * 8 * CW, 8 * CW)) for rc in range(2)]
                Qr = View(qr.t[0:64, 0:8 * CW].rearrange("p (h w) -> p h w", h=8)[:, :, bp * 8:bp * 8 + 8], qr.pages(0, 8 * CW))
                return Q, Qr
            steps = []
            for bp in range(8):
                for g in range(16):
                    for half in range(2):
                        steps.append((bp, g, half))
                steps.append((bp, -1, 0))

            def gather(bp, g):
                gi = bp * 16 + g
                Gb, GRb = G[gi % NB], GR[gi % NB]
                self.op("pool", lambda e: e.indirect_dma_start(
                    out=Gb.t[:, 0:2048], out_offset=None, in_=d["clat"],
                    in_offset=bass.IndirectOffsetOnAxis(ap=self.idxb.t[:, gi:gi + 1], axis=0)),
                    r=[ptv], w=[Gb.v()], dma=self.dsem(f"G{gi % NB}"))
                self.op("pool", lambda e: e.indirect_dma_start(
                    out=GRb.t[:, 0:512], out_offset=None, in_=d["ckr"],
                    in_offset=bass.IndirectOffsetOnAxis(ap=self.idxb.t[:, gi:gi + 1], axis=0)),
                    r=[ptv], w=[GRb.v()], dma=self.dsem(f"GR{gi % NB}"))

            def a_step(i):
                bp, g, half = steps[i]
                if g < 0:
                    return
                gi = bp * 16 + g
                if half == 0 and gi + 1 < 128:
                    gather((gi + 1) // 16, (gi + 1) % 16)
                Gb, GRb = G[gi % NB], GR[gi % NB]
                kt_, krt_ = KT[i % NB], KRT[i % NB]
                for t4 in range(4):
                    t = half * 4 + t4
                    for rc in range(2):
                        self.tr(self.pstv(rc * 512 + t4 * 128, 128), Gb.v(t * 256 + rc * 128, 128))
                    self.tr(self.pstv(1024 + t4 * 128, 128, parts=(0, 64)), GRb.v(t * 64, 64))
                self.acopy(kt_.v(0, 1024), self.pstv(0, 1024))
                self.vcopy(krt_.v(0, 512, parts=(0, 64)), self.pstv(1024, 512, parts=(0, 64)))

            def b_step(i):
                bp, g, half = steps[i]
                Q, Qr = qviews(bp)
                if g < 0:
                    Sn = self.bank(3 + i % 3, 0, 64, parts=(0, 64))
                    pairs = [(cT.v(rc * L + SEQ, 64), Q[rc]) for rc in range(2)]
                    pairs.append((krT.v(SEQ, 64, parts=(0, 64)), Qr))
                    pairs.append((self.cmv(CM_ID, 64, parts=(0, 64)), self.cmv(CM_NEWM + bp * 64, 64, parts=(0, 64))))
                    self.mm(Sn, pairs)
                    expo(PT3[i % NB].v(0, 64, parts=(0, 64)), Sn)
                    return
                kt_, krt_ = KT[i % NB], KRT[i % NB]
                S = self.bank(3 + i % 3, 0, 256)
                for t4 in range(4):
                    pairs = [(kt_.v(rc * 512 + t4 * 128, 128), Q[rc]) for rc in range(2)]
                    pairs.append((krt_.v(t4 * 128, 128, parts=(0, 64)), Qr))
                    pairs.append((ident, cross))
                    self.mm(S.sub(S.ap[:, t4 * 64:(t4 + 1) * 64]), pairs)
                expo(PT3[i % NB].v(0, 256), S)

            def c_step(i):
                bp, g, half = steps[i]
                if g < 0:
                    ptn = PT3[i % NB].v(0, 64, parts=(0, 64))
                    for rc in range(2):
                        self.mm(O[rc], [(ctok.v(32 * 256 + rc * 128, 128, parts=(0, 64)), ptn)], start=False, stop=True)
                    self.mm(DN, [(self.cmv(CM_ONES, 128, parts=(0, 64)), ptn)], start=False, stop=True)
                    rcv = rec.v(0, 64)
                    self.op("dve", lambda e: e.reciprocal(out=rcv.ap, in_=DN.ap), r=[DN], w=[rcv])
                    for rc in range(2):
                        ov = View(ql.t[:, rc * 8 * CW:(rc * 8 + 8) * CW].rearrange("p (h w) -> p h w", h=8)[:, :, bp * 8:bp * 8 + 8],
                                  ql.pages(rc * 8 * CW, 8 * CW))
                        Ov = O[rc].sub(O[rc].ap.rearrange("p (h c) -> p h c", h=8))
                        rr = rcv.sub(rcv.ap.rearrange("p (h c) -> p h c", h=8))
                        self.vtt(ov, Ov, rr, ALU.mult)
                    return
                gi = bp * 16 + g
                Gb = G[gi % NB]
                pt = PT3[i % NB].v(0, 256)
                for t4 in range(4):
                    t = half * 4 + t4
                    first = (g == 0 and half == 0 and t4 == 0)
                    pc = pt.sub(pt.ap[:, t4 * 64:(t4 + 1) * 64])
                    for rc in range(2):
                        self.mm(O[rc], [(Gb.v(t * 256 + rc * 128, 128), pc)], start=first, stop=False)
                    self.mm(DN, [(ones, pc)], start=first, stop=False)
            n = len(steps)
            gather(0, 0)
            a_step(0)
            a_step(1)
            b_step(0)
            for i in range(n):
                if i + 2 < n:
                    a_step(i + 2)
                if i + 1 < n:
                    b_step(i + 1)
                c_step(i)
        if kind == "p" and self.debug:
            qv_all = ql.v(0, 16 * CW)
            self.op("pool", lambda e: e.dma_start(out=d["dbg"], in_=qv_all.ap), r=[qv_all], dma=self.dsem("dbg"))
        wuv = self.ws.get(d["m_uv"][0], 2048)
        for h in range(8):
            pa = nb()
            self.mm(pa, [(wuv.v((h * 2 + rc) * 128, 128), ql.v((rc * 8 + h) * CW, NT)) for rc in range(2)])
            self.acopy(ob.v(h * CW, NT), pa)
        for u in range(2):
            wb = self.ws.get(d["m_o"][u], 4096)
            for o4 in range(4):
                dc = 4 * u + o4
                po = nb()
                self.mm(po, [(wb.v(k * 512 + o4 * 128, 128), ob.v(k * CW, NT)) for k in range(KC)])
                x = self.xc(dc)
                self.vtt(x, x, po, ALU.add)


    def ret_layer(self):
        NT, d, kind, ti = self.NT, self.d, self.kind, self.ti
        CW = NT
        t0 = ti * NTP if kind == "p" else SEQ
        smp = kind == "s"
        Sst = self.Sst
        ar = self.arena(0)
        qT = ar.take("r_qT", 8 * CW, BF16)
        kT = ar.take("r_kT", 8 * CW, BF16)
        gated = ar.take("r_gated", 16 * CW, BF16)
        e0 = ar.off
        ropet = ar.take("r_rope", 2 * CW, F32)
        t1 = ar.take("r_t1", CW, F32)
        t2 = ar.take("r_t2", CW, F32)
        ar2 = self.arena(e0)
        vtok = ar2.take("r_vtok", 4 * 512, BF16)
        innT = ar2.take("r_innT", 4 * CW, BF16)
        ohb = ar2.take("r_ohb", 4 * CW, BF16)
        ohsq = ar2.take("r_ohsq", 4 * CW, BF16)
        kdtok = ar2.take("r_kdtok", 4 * 256, BF16)
        Sbf = [ar2.take(f"r_Sbf{i}", 2 * 512, BF16) for i in range(2 if smp else 1)]
        sgt = [ar2.take(f"r_sgt{i}", CW, BF16) for i in range(2)]
        mub = ar2.take("r_mu", CW, F32)
        rsb = ar2.take("r_rs", CW, F32)
        tt = [ar2.take("r_tt0", CW, BF16)] * 2
        if smp:
            Sin = [ar2.take(f"r_Sin{i}", 2 * 512, F32) for i in range(2)]
            Sout = [ar2.take(f"r_Sout{i}", 2 * 512, F32) for i in range(2)]
            kdm = [ar2.take(f"r_kdm{i}", 256, BF16) for i in range(2)]
        bk = 0

        def nb(n=None, parts=None):
            nonlocal bk
            b = self.bank(bk % 3, 0, NT if n is None else n, parts)
            bk += 1
            return b
        self.op("sp", lambda e: e.dma_start(out=ropet.t[:, 0:2 * CW].rearrange("p (a w) -> p a w", a=2)[:, :, 0:NT],
                                            in_=d["roper"][:, :, t0:t0 + NT]), w=[ropet.v()], dma=self.dsem("roper"))
        cos, sin = ropet.v(0, NT), ropet.v(CW, NT)
        a, b = t1.v(0, NT), t2.v(0, NT)
        gq0, gk0 = (CM_GQS, CM_GKS) if smp else (CM_GQ, CM_GK)
        gw = 64 if smp else 512
        for name, dst, g0 in (("r_q", qT, gq0), ("r_k", kT, gk0)):
            for u in range(2):
                wb = self.ws.get(d[name][u], 4096)
                for h2 in range(2):
                    h = 2 * u + h2
                    pa, pb = nb(), nb()
                    self.mm(pa, [(wb.v(k * 512 + (2 * h2) * 128, 128), self.hc(k)) for k in range(KC)])
                    self.mm(pb, [(wb.v(k * 512 + (2 * h2 + 1) * 128, 128), self.hc(k)) for k in range(KC)])
                    gt = self.cmv(g0 + h * gw, NT)
                    o1, o2 = dst.v((2 * h) * CW, NT), dst.v((2 * h + 1) * CW, NT)
                    self.vtt(a, pa, cos, ALU.mult)
                    self.vtt(b, pb, sin, ALU.mult)
                    self.vtt(a, a, b, ALU.subtract)
                    self.vtt(o1, a, gt, ALU.mult)
                    self.vtt(a, pb, cos, ALU.mult)
                    self.vtt(b, pa, sin, ALU.mult)
                    self.vtt(a, a, b, ALU.add)
                    self.vtt(o2, a, gt, ALU.mult)
        mu, rs = mub.v(0, NT), rsb.v(0, NT)
        nmb = 1 if smp else 4
        MT = 64 if smp else 128
        pr = (0, 64) if smp else None
        for h in range(4):
            gam = GAMMAS[h]
            wv = self.ws.get(d["r_v"][h], 4096)
            for mb in range(nmb):
                pv_ = self.bank(bk % 3, 0, 512, pr)
                bk += 1
                self.mm(pv_, [(self.hT.v(k * NTP + mb * MT, MT), wv.v(k * 512, 512)) for k in range(KC)])
                self.acopy(vtok.v(mb * 512, 512, parts=pr), pv_)
            for mb in range(nmb):
                pi = self.bank(bk % 3, 0, NT, pr)
                bk += 1
                self.mm(pi, [(kT.v((2 * h + dc) * CW + mb * MT, MT), qT.v((2 * h + dc) * CW, NT)) for dc in range(2)])
                mk = self.cmv(CM_M01S, 64, parts=(0, 64)) if smp else self.cmv(CM_MASK01 + mb * 512, 512)
                self.vtt(innT.v(mb * CW, NT, parts=pr), pi, mk, ALU.mult)
            for mb in range(nmb):
                for dc in range(2):
                    self.tr(self.pstv((mb * 2 + dc) * 128, 128, parts=pr), kT.v((2 * h + dc) * CW + mb * MT, MT))
            self.acopy(kdtok.v(0, nmb * 256, parts=pr), self.pstv(0, nmb * 256, parts=pr), scale=float(gam ** (3 if smp else 511)))
            if not smp:
                sb = Sbf[0]
                for dc in range(2):
                    self.acopy(sb.v(dc * 512, 512), Sst.v((h * 2 + dc) * 512, 512), scale=float(gam))
                for ec in range(4):
                    po = nb()
                    pairs = [(vtok.v(mb * 512 + ec * 128, 128), innT.v(mb * CW, NT)) for mb in range(4)]
                    pairs += [(sb.v(dc * 512 + ec * 128, 128), qT.v((2 * h + dc) * CW, NT)) for dc in range(2)]
                    self.mm(po, pairs)
                    self.acopy(ohb.v(ec * CW, NT), po)
                    o_ = ohsq.v(ec * CW, NT)
                    self.op("act", lambda e, o_=o_, po=po: e.activation(out=o_.ap, in_=po.ap, func=AF.Square), r=[po], w=[o_])
                for dc in range(2):
                    pu = self.bank(bk % 3, 0, 512)
                    bk += 1
                    self.mm(pu, [(kdtok.v(mb * 256 + dc * 128, 128), vtok.v(mb * 512, 512)) for mb in range(4)])
                    sv = Sst.v((h * 2 + dc) * 512, 512)
                    self.op("dve", lambda e, sv=sv, pu=pu, gam=gam: e.scalar_tensor_tensor(
                        out=sv.ap, in0=sv.ap, scalar=float(gam ** 512), in1=pu.ap, op0=ALU.mult, op1=ALU.add), r=[sv, pu], w=[sv])
            else:
                PO = [self.bank(2 + ec, 0, 64) for ec in range(4)]
                for ec in range(4):
                    self.mm(PO[ec], [(vtok.v(ec * 128, 128, parts=(0, 64)), innT.v(0, NT, parts=(0, 64)))], start=True, stop=False)
                def ld(bi, h=h):
                    siv = Sin[bi % 2].v3(0, 2, 512)
                    self.op("sp", lambda e: e.dma_start(out=siv.ap, in_=d["rsin"][bi, :, 2 * h:2 * h + 2, :]),
                            w=[siv], dma=self.dsem(f"rsin{bi % 2}"))
                ld(0)
                for bi in range(16):
                    si, so, sb, km = Sin[bi % 2], Sout[bi % 2], Sbf[bi % 2], kdm[bi % 2]
                    if bi + 1 < 16:
                        ld(bi + 1)
                    self.acopy(sb.v(0, 1024), si.v(0, 1024), scale=float(gam))
                    for ec in range(4):
                        pc = PO[ec].sub(PO[ec].ap[:, bi * 4:bi * 4 + 4])
                        self.mm(pc, [(sb.v(dc * 512 + ec * 128, 128), qT.v((2 * h + dc) * CW + bi * 4, 4)) for dc in range(2)],
                                start=False, stop=(bi == 15))
                    oh_ = self.pv("onehot", bi)
                    kmv = km.v(0, 256, parts=(0, 64))
                    kdv = kdtok.v(0, 256, parts=(0, 64))
                    ohp = oh_.sub(oh_.ap[0:64])
                    self.op("dve", lambda e, kmv=kmv, kdv=kdv, ohp=ohp: e.tensor_scalar(
                        out=kmv.ap, in0=kdv.ap, scalar1=ohp.ap, scalar2=None, op0=ALU.mult), r=[kdv, ohp], w=[kmv])
                    for dc in range(2):
                        pu = self.bank(bk % 2, 0, 512)
                        bk += 1
                        self.mm(pu, [(km.v(dc * 128, 128, parts=(0, 64)), vtok.v(0, 512, parts=(0, 64)))])
                        sv, ov = si.v(dc * 512, 512), so.v(dc * 512, 512)
                        self.op("dve", lambda e, sv=sv, ov=ov, pu=pu, gam=gam: e.scalar_tensor_tensor(
                            out=ov.ap, in0=sv.ap, scalar=float(gam ** 4), in1=pu.ap, op0=ALU.mult, op1=ALU.add), r=[sv, pu], w=[ov])
                    sov = so.v3(0, 2, 512)
                    self.op("sp", lambda e, sov=sov, bi=bi, h=h: e.dma_start(out=d["rss"][bi, :, 2 * h:2 * h + 2, :], in_=sov.ap),
                            r=[sov], dma=self.dsem(f"rss{bi % 2}"))
                for ec in range(4):
                    self.acopy(ohb.v(ec * CW, NT), PO[ec])
                    o_ = ohsq.v(ec * CW, NT)
                    self.op("act", lambda e, o_=o_, po=PO[ec]: e.activation(out=o_.ap, in_=po.ap, func=AF.Square), r=[PO[ec]], w=[o_])
            mean = self.stats([ohb.v(ec * CW, NT) for ec in range(4)], CM_M512, 5)
            msq = self.stats([ohsq.v(ec * CW, NT) for ec in range(4)], CM_M512, 4)
            self.acopy(mu, mean)
            self.vtt(rs, mu, mu, ALU.mult)
            self.vtt(rs, msq, rs, ALU.subtract)
            self.rsqrt(rs, rs)
            wg = self.ws.get(d["r_g"][h], 4096)
            for ec in range(4):
                c16 = 4 * h + ec
                pg = nb()
                self.mm(pg, [(wg.v(k * 512 + ec * 128, 128), self.hc(k)) for k in range(KC)])
                sg_ = sgt[ec % 2].v(0, NT)
                self.op("act", lambda e, sg_=sg_, pg=pg: e.activation(out=sg_.ap, in_=pg.ap, func=AF.Silu), r=[pg], w=[sg_])
                t = tt[ec % 2].v(0, NT)
                self.vtt(t, ohb.v(ec * CW, NT), mu, ALU.subtract)
                self.vtt(t, t, rs, ALU.mult)
                gg, gb = self.pv("ret_gn_g", c16), self.pv("ret_gn_b", c16)
                self.op("act", lambda e, t=t, gg=gg, gb=gb: e.activation(out=t.ap, in_=t.ap, func=AF.Identity, scale=gg.ap, bias=gb.ap),
                        r=[t, gg, gb], w=[t])
                self.vtt(gated.v(c16 * CW, NT), t, sg_, ALU.mult)
        for u in range(4):
            wb = self.ws.get(d["r_o"][u], 4096)
            for o2 in range(2):
                dc = 2 * u + o2
                po = nb()
                self.mm(po, [(wb.v(k * 256 + o2 * 128, 128), gated.v(k * CW, NT)) for k in range(16)])
                x = self.xc(dc)
                self.vtt(x, x, po, ALU.add)
        if kind == "p" and ti == SEQ // NTP - 1:
            sv = Sst.v3(0, 8, 512)
            self.op("sp", lambda e: e.dma_start(out=d["rsp"], in_=sv.ap), r=[sv], dma=self.dsem("rsp"))


_PROG = {}


def get_prog(n_tiles=8, depth=4, sample=True):
    key = (n_tiles, depth, sample)
    if key not in _PROG:
        _PROG[key] = Prog(n_tiles, depth, sample)
    return _PROG[key]


def core_inputs(inp, w, consts, core):
    cm, ropem, roper = consts
    b = core % 4
    m = dict(w)
    m["cm"], m["ropem"], m["roper"] = cm, ropem, roper
    m["xp"] = _fm(np.asarray(inp["x_prompt"][b], np.float32))
    xs = np.asarray(inp["x_sample"][16 * core:16 * core + 16], np.float32).reshape(NS, D)
    m["xs"] = _fm(xs)
    m["clat"] = np.asarray(inp["cache_mla_latent"][0], np.float32).reshape(10240 * 16, 2048)
    m["ckr"] = np.asarray(inp["cache_mla_krope"][0], np.float32).reshape(10240 * 16, 512)
    ptc = np.asarray(inp["page_table"][16 * core:16 * core + 16], np.int32)
    m["pt"] = np.ascontiguousarray(ptc.reshape(8, 128).T)
    sr = np.asarray(inp["state_ret"][0, 16 * core:16 * core + 16], np.float32)
    m["rsin"] = np.ascontiguousarray(sr.reshape(16, 4, 2, P, 512).transpose(0, 3, 1, 2, 4).reshape(16, P, 8, 512))
    sc = np.asarray(inp["state_conv"][:, 16 * core:16 * core + 16], np.float32)
    m["sconv"] = np.ascontiguousarray(sc.reshape(2, 16, 30, KC, P).transpose(0, 4, 3, 1, 2).reshape(2, P, KC, 480))
    return m


def kernel(**inp):
    prog = get_prog()
    w = host_weights(inp)
    consts = host_consts()
    in_maps = [core_inputs(inp, w, consts, c) for c in range(N_CORES)]
    res = run_bass_kernel_spmd(prog.nc, in_maps, core_ids=list(range(N_CORES)))
    return assemble(res.results)


def assemble(R):
    yp = np.stack([_unfm(R[b]["yp"]) for b in range(4)])
    ys = np.concatenate([_unfm(R[c]["ys"]).reshape(16, 4, D) for c in range(N_CORES)])
    cvp = np.stack([np.stack([_unfm(R[b]["cvp"][j]) for b in range(4)]) for j in range(2)])
    cvs = np.stack([np.concatenate([R[c]["cvs"][j].reshape(P, KC, 16, 30).transpose(2, 3, 1, 0).reshape(16, 30, D)
                                    for c in range(N_CORES)]) for j in range(2)])
    latp = np.stack([_unfm(R[b]["latp"]) for b in range(4)])[None]
    krp = np.stack([np.ascontiguousarray(R[b]["krp"].T) for b in range(4)])[None]
    lats = np.concatenate([_unfm(R[c]["lats"]).reshape(16, 4, 256) for c in range(N_CORES)])[None]
    krs = np.concatenate([np.ascontiguousarray(R[c]["krs"].T).reshape(16, 4, 64) for c in range(N_CORES)])[None]
    rsp = np.stack([R[b]["rsp"].reshape(P, 4, 2, 512).transpose(1, 2, 0, 3).reshape(4, 256, 512) for b in range(4)])[None]
    rss = np.concatenate([R[c]["rss"].reshape(16, P, 4, 2, 512).transpose(0, 2, 3, 1, 4).reshape(16, 4, 256, 512)
                          for c in range(N_CORES)])[None]
    return (yp, ys, cvp, cvs, latp, krp, lats, krs, rsp, rss)
```
